# Optimizing a Trainium2 kernel written in Bass

```python
import math
import jax, jax.numpy as jnp
from jax import lax
import numpy as np

D_MODEL = 2048
BATCH = 1
SEQ = 8192
DEPTH = 2

N_EVEN = (DEPTH + 1) // 2
N_ODD = DEPTH // 2
D_HY = D_MODEL // 2
D_SC = D_MODEL // 2
HY_STREAMS = 3
SC_STREAMS = 3
D_IN = HY_STREAMS * D_HY + SC_STREAMS * D_SC
SHORT_W = 3
FILTER_BANDS = 16
EMB_DIM = 1 + 2 * FILTER_BANDS
FILTER_HIDDEN = 64
DECAY_TARGET = 1e-2
FAST_DECAY_PCT = 0.3
SLOW_DECAY_PCT = 1.5
POOL_WINDOWS = (2, 4, 8, 16)
N_POOL_GROUPS = len(POOL_WINDOWS)
D_POOL_G = D_MODEL // N_POOL_GROUPS
D_FF = 4 * D_MODEL
NORM_EPS = 1e-6

kernel_name = "hybrid_hyena_shortconv_pool_encoder"


def rmsnorm(u, g):
    u32 = u.astype(jnp.float32)
    r = lax.rsqrt(jnp.mean(u32 * u32, axis=-1, keepdims=True) + NORM_EPS)
    return (u32 * r * g.astype(jnp.float32)).astype(u.dtype)


def dwconv3(u, w, b=None):
    up = jnp.pad(u, ((0, 0), (1, 1), (0, 0)))
    y = up[:, :-2] * w[0] + up[:, 1:-1] * w[1] + up[:, 2:] * w[2]
    return y if b is None else y + b


def hyena_filters(L, w1, b1, w2, b2, w3, b3, freq):
    f32 = jnp.float32
    t = jnp.linspace(0.0, 1.0, L, dtype=f32)[:, None]
    w_pos = 2.0 * math.pi * jnp.arange(L, dtype=f32)[:, None] / L
    bands = jnp.linspace(1e-4, FILTER_BANDS - 1, FILTER_BANDS, dtype=f32)[None, :]
    ang = w_pos * bands
    emb = jnp.concatenate([t, jnp.cos(ang), -jnp.sin(ang)], axis=-1)
    fr = freq.astype(f32)
    h = jnp.sin(fr * (emb @ w1.astype(f32) + b1.astype(f32)))
    h = jnp.sin(fr * (h @ w2.astype(f32) + b2.astype(f32)))
    h = h @ w3.astype(f32) + b3.astype(f32)
    max_decay = math.log(DECAY_TARGET) / FAST_DECAY_PCT
    min_decay = math.log(DECAY_TARGET) / SLOW_DECAY_PCT
    deltas = jnp.linspace(min_decay, max_decay, D_HY, dtype=f32)[None, :]
    window = jnp.exp(-t * jnp.abs(deltas))
    return h[:, :D_HY] * window, h[:, D_HY:] * window


def two_sided_fftconv(v, h_fwd, h_bwd, bias):
    L = v.shape[1]
    v32 = v.astype(jnp.float32)
    k = jnp.concatenate([h_fwd, jnp.zeros((1, D_HY), jnp.float32), h_bwd[:0:-1]], axis=0)
    V = jnp.fft.rfft(v32, n=2 * L, axis=1)
    K = jnp.fft.rfft(k, n=2 * L, axis=0)
    y = jnp.fft.irfft(V * K[None], n=2 * L, axis=1)[:, :L]
    return (y + v32 * bias.astype(jnp.float32)).astype(v.dtype)


def hyena_shortconv_mixer(h, w_in, hy_short_w, hy_short_b, f_w1, f_b1, f_w2, f_b2, f_w3, f_b3,
                          freq, hy_bias, sc_conv_w, w_out):
    L = h.shape[1]
    z = h @ w_in
    hy = dwconv3(z[..., :HY_STREAMS * D_HY], hy_short_w, hy_short_b)
    x0, x1, v = jnp.split(hy, HY_STREAMS, axis=-1)
    h_fwd, h_bwd = hyena_filters(L, f_w1, f_b1, f_w2, f_b2, f_w3, f_b3, freq)
    y_a = x0 * two_sided_fftconv(v * x1, h_fwd, h_bwd, hy_bias)
    gb, gc, xv = jnp.split(z[..., HY_STREAMS * D_HY:], SC_STREAMS, axis=-1)
    y_b = gb * dwconv3(gc * xv, sc_conv_w)
    return jnp.concatenate([y_a, y_b], axis=-1) @ w_out


def centred_mean(u, r):
    L = u.shape[1]
    cs = jnp.pad(lax.cumsum(u.astype(jnp.float32), axis=1), ((0, 0), (1, 0), (0, 0)))
    t = jnp.arange(L)
    lo = jnp.maximum(t - r, 0)
    hi = jnp.minimum(t + r + 1, L)
    s = cs[:, hi] - cs[:, lo]
    cnt = (hi - lo).astype(jnp.float32)[None, :, None]
    return (s / cnt).astype(u.dtype)


def pool_mixer(h, pool_w, pool_scale):
    outs = []
    for g, w in enumerate(POOL_WINDOWS):
        u = h[..., g * D_POOL_G:(g + 1) * D_POOL_G]
        d = centred_mean(u, w // 2) - u
        outs.append(d @ pool_w[g])
    return jnp.concatenate(outs, axis=-1) * pool_scale


def setup_inputs(seed: int = 0) -> dict:
    key = jax.random.key(seed)
    ks = jax.random.split(key, 20)
    nrm = lambda k, shape, s: jax.random.normal(k, shape, jnp.float32) * s
    return {
        "x": nrm(ks[0], (BATCH, SEQ, D_MODEL), 1.0),
        "norm_g": 1.0 + nrm(ks[1], (DEPTH, 4, D_MODEL), 0.05),
        "mix_w_in": nrm(ks[2], (N_EVEN, D_MODEL, D_IN), D_MODEL ** -0.5),
        "hy_short_w": nrm(ks[3], (N_EVEN, SHORT_W, HY_STREAMS * D_HY), SHORT_W ** -0.5),
        "hy_short_b": nrm(ks[4], (N_EVEN, HY_STREAMS * D_HY), 0.02),
        "hy_filt_w1": nrm(ks[5], (N_EVEN, EMB_DIM, FILTER_HIDDEN), EMB_DIM ** -0.5),
        "hy_filt_b1": nrm(ks[6], (N_EVEN, FILTER_HIDDEN), 0.02),
        "hy_filt_w2": nrm(ks[7], (N_EVEN, FILTER_HIDDEN, FILTER_HIDDEN), FILTER_HIDDEN ** -0.5),
        "hy_filt_b2": nrm(ks[8], (N_EVEN, FILTER_HIDDEN), 0.02),
        "hy_filt_w3": nrm(ks[9], (N_EVEN, FILTER_HIDDEN, 2 * D_HY), FILTER_HIDDEN ** -0.5),
        "hy_filt_b3": nrm(ks[10], (N_EVEN, 2 * D_HY), 0.02),
        "hy_freq": 1.0 + nrm(ks[11], (N_EVEN, FILTER_HIDDEN), 0.05),
        "hy_bias": nrm(ks[12], (N_EVEN, D_HY), 0.5),
        "sc_conv_w": nrm(ks[13], (N_EVEN, SHORT_W, D_SC), SHORT_W ** -0.5),
        "mix_w_out": nrm(ks[14], (N_EVEN, D_HY + D_SC, D_MODEL), (D_HY + D_SC) ** -0.5),
        "pool_w": nrm(ks[15], (N_ODD, N_POOL_GROUPS, D_POOL_G, D_POOL_G), D_POOL_G ** -0.5),
        "pool_scale": 1.0 + nrm(ks[16], (N_ODD, D_MODEL), 0.1),
        "mlp_w1": nrm(ks[17], (DEPTH, D_MODEL, D_FF), D_MODEL ** -0.5),
        "mlp_w2": nrm(ks[18], (DEPTH, D_FF, D_MODEL), D_FF ** -0.5),
    }


def reference(x, norm_g, mix_w_in, hy_short_w, hy_short_b, hy_filt_w1, hy_filt_b1, hy_filt_w2,
              hy_filt_b2, hy_filt_w3, hy_filt_b3, hy_freq, hy_bias, sc_conv_w, mix_w_out,
              pool_w, pool_scale, mlp_w1, mlp_w2):
    for i in range(DEPTH):
        g = norm_g[i]
        j = i // 2
        h = rmsnorm(x, g[0])
        if i % 2 == 0:
            m = hyena_shortconv_mixer(h, mix_w_in[j], hy_short_w[j], hy_short_b[j],
                                      hy_filt_w1[j], hy_filt_b1[j], hy_filt_w2[j], hy_filt_b2[j],
                                      hy_filt_w3[j], hy_filt_b3[j], hy_freq[j], hy_bias[j],
                                      sc_conv_w[j], mix_w_out[j])
        else:
            m = pool_mixer(h, pool_w[j], pool_scale[j])
        x = x + rmsnorm(m, g[1])
        h = rmsnorm(x, g[2])
        f = jnp.square(jax.nn.relu(h @ mlp_w1[i])) @ mlp_w2[i]
        x = x + rmsnorm(f, g[3])
    return x
```

```python
import contextlib, math
import numpy as np
import ml_dtypes
import concourse.bass as bass
import concourse.mybir as mybir
from concourse.bass_utils import run_bass_kernel_spmd

F32 = mybir.dt.float32
BF16 = mybir.dt.bfloat16
ALU = mybir.AluOpType
AF = mybir.ActivationFunctionType

D = 2048; L = 8192; NC = 8; TL = L // NC; DFF = 8192; EPS = 1e-6
KC = D // 128


class Buf:
    __slots__ = ("name", "w", "rs")

    def __init__(self, name=""):
        self.name = name; self.w = None; self.rs = []


class Op:
    __slots__ = ("eng", "fn", "deps", "sig", "chan", "done", "waits")


class Sched:
    ENG = ("pe", "act", "dve", "pool", "sp")

    def __init__(self, nc):
        self.nc = nc
        self.ops = {e: [] for e in self.ENG}
        self.chan_cnt = {}
        self.chan_last = {}
        self.bar = {e: None for e in self.ENG}

    def op(self, eng, fn, reads=(), writes=(), chan=None):
        o = Op(); o.eng = eng; o.fn = fn; o.sig = False; o.chan = chan; o.done = None; o.waits = []
        deps = set()
        for b in reads:
            if b.w is not None:
                deps.add(b.w)
        for b in writes:
            cands = ([b.w] if b.w is not None else []) + b.rs
            for d in cands:
                if d is o:
                    continue
                if d.eng != eng or d.chan is not None or chan is not None:
                    deps.add(d)
        if self.bar[eng] is not None:
            deps.update(self.bar[eng]); self.bar[eng] = None
        deps.discard(o)
        o.deps = deps
        for b in reads:
            b.rs.append(o)
        for b in writes:
            b.w = o; b.rs = []
        if chan is not None:
            self.chan_cnt[chan] = self.chan_cnt.get(chan, 0) + 1
            o.done = ("c_" + chan, 16 * self.chan_cnt[chan])
            self.chan_last[chan] = o
        self.ops[eng].append(o)
        return o

    def barrier(self):
        last = [self.ops[e][-1] for e in self.ENG if self.ops[e]] + list(self.chan_last.values())
        for e in self.ENG:
            self.bar[e] = set(last)

    def finish(self, eng="sp"):
        self.barrier()
        self.op(eng, None)

    def emit(self):
        nc = self.nc
        for e in self.ENG:
            for o in self.ops[e]:
                for d in o.deps:
                    d.sig = True
        for e in self.ENG:
            cnt = 0
            for o in self.ops[e]:
                if o.chan is None and o.sig:
                    cnt += 1; o.done = ("e_" + e, cnt)
        names = set()
        for e in self.ENG:
            seen = {}
            for o in self.ops[e]:
                w = {}
                for d in o.deps:
                    k, v = d.done
                    w[k] = max(w.get(k, 0), v)
                o.waits = [(k, v) for k, v in sorted(w.items()) if seen.get(k, 0) < v]
                for k, v in o.waits:
                    seen[k] = v; names.add(k)
                if o.done is not None:
                    names.add(o.done[0])
        with contextlib.ExitStack() as es:
            sems = {k: es.enter_context(nc.semaphore(k)) for k in sorted(names)}

            def run(e, eng):
                for o in self.ops[e]:
                    for k, v in o.waits:
                        eng.wait_ge(sems[k], v)
                    if o.fn is None:
                        continue
                    ins = o.fn(eng)
                    if o.chan is not None:
                        ins.then_inc(sems[o.done[0]], 16)
                    elif o.sig:
                        ins.then_inc(sems[o.done[0]], 1)

            with nc.Block() as block:
                @block.tensor
                def _(eng):
                    run("pe", eng)

                @block.scalar
                def _(eng):
                    run("act", eng)

                @block.vector
                def _(eng):
                    run("dve", eng)

                @block.gpsimd
                def _(eng):
                    run("pool", eng)

                @block.sync
                def _(eng):
                    run("sp", eng)


class Rot:
    def __init__(self, items):
        self.items = items; self.i = 0

    def next(self):
        it = self.items[self.i % len(self.items)]; self.i += 1
        return it


class Ctx:
    def __init__(self):
        self.nc = bass.Bass("TRN2", target_bir_lowering=False)
        self.S = Sched(self.nc)
        self.es = contextlib.ExitStack()
        self.n = 0

    def sb(self, shape, dt, name=None):
        self.n += 1
        t = self.es.enter_context(self.nc.sbuf_tensor("s_" + (name or f"sb{self.n}"), list(shape), dt))
        return t

    def ps(self, name=None):
        self.n += 1
        return self.es.enter_context(self.nc.psum_tensor("p_" + (name or f"ps{self.n}"), [128, 512], F32))

    def din(self, name, shape, dt=F32):
        return self.nc.dram_tensor(name, list(shape), dt, kind="ExternalInput").ap()

    def dout(self, name, shape, dt=F32):
        return self.nc.dram_tensor(name, list(shape), dt, kind="ExternalOutput").ap()

    def dscratch(self, name, shape, dt):
        return self.nc.dram_tensor(name, list(shape), dt).ap()

    def rot(self, n, shape, dt, name):
        return Rot([(self.sb(shape, dt, f"{name}{i}"), Buf(f"{name}{i}")) for i in range(n)])

    def psrot(self, n, name):
        return Rot([(self.ps(f"{name}{i}"), Buf(f"{name}{i}")) for i in range(n)])

    def dma(self, q, out, in_, reads, writes, chan):
        return self.S.op(q, lambda e: e.dma_start(out=out, in_=in_), reads, writes, chan=chan)

    def done(self):
        self.S.finish("sp")
        self.S.emit()
        self.es.close()
        return self.nc


def cchunks(n, m=512):
    out = []; a = 0
    k = (n + m - 1) // m
    base = n // k; rem = n % k
    for i in range(k):
        w = base + (1 if i < rem else 0)
        out.append((a, w)); a += w
    return out


def emit_consts(C):
    S = C.S
    ones = C.sb([128, 128], BF16, "ones"); b_ones = Buf("ones")
    S.op("dve", lambda e: e.memset(ones[:], 1.0), [], [b_ones])
    return ones, b_ones


def emit_stats(C, ones, b_ones, src_chunks, ncols, rbc, b_rbc, sqrot, psrot):
    S = C.S
    for (c0, w) in cchunks(ncols):
        pt, pb = psrot.next()
        n = len(src_chunks)
        for k, (apf, sb_) in enumerate(src_chunks):
            sq, sqb = sqrot.next()
            S.op("act", lambda e, sq=sq, apf=apf, c0=c0, w=w: e.activation(out=sq[:, :w], in_=apf(c0, w), func=AF.Square),
                 [sb_], [sqb])
            S.op("pe", lambda e, pt=pt, sq=sq, w=w, k=k, n=n: e.matmul(pt[:, :w], ones[:], sq[:, :w], start=(k == 0), stop=(k == n - 1)),
                 [sqb, b_ones], [pb])
        S.op("dve", lambda e, pt=pt, c0=c0, w=w: e.tensor_scalar(out=rbc[:, c0:c0 + w], in0=pt[:, :w], scalar1=1.0 / D, scalar2=EPS,
                                                                op0=ALU.mult, op1=ALU.add), [pb], [b_rbc])
    S.op("act", lambda e: e.activation(out=rbc[:, :ncols], in_=rbc[:, :ncols], func=AF.Sqrt), [b_rbc], [b_rbc])
    S.op("dve", lambda e: e.reciprocal(out=rbc[:, :ncols], in_=rbc[:, :ncols]), [b_rbc], [b_rbc])


def emit_post(C, ones, b_ones, ACC, b_acc, ncols_acc, own0, gpost, b_g, xsrc, outT, sqrot, psrot, rbc, b_rbc):
    S = C.S
    chunks = [((lambda c0, w, k=k: ACC[:, k, own0 + c0:own0 + c0 + w]), b_acc[k]) for k in range(KC)]
    emit_stats(C, ones, b_ones, chunks, TL, rbc, b_rbc, sqrot, psrot)
    stg = C.rot(2, [128, 512], F32, "pstg")
    otm = C.rot(2, [128, 512], F32, "potm")
    n = 0
    for k in range(KC):
        for tc in range(TL // 512):
            st, stb = stg.next(); ot, otb = otm.next()
            a = own0 + tc * 512
            C.dma("sp", st[:], xsrc(k)[:, tc * 512:(tc + 1) * 512], [], [stb], chan=f"pstg{n % 2}")
            S.op("dve", lambda e, k=k, ot=ot, a=a, tc=tc: e.scalar_tensor_tensor(out=ot[:], in0=ACC[:, k, a:a + 512], scalar=gpost[:, k:k + 1],
                                                                           in1=rbc[:, tc * 512:(tc + 1) * 512], op0=ALU.mult, op1=ALU.mult),
                 [b_acc[k], b_g, b_rbc], [otb])
            S.op("pool", lambda e, ot=ot, st=st: e.tensor_tensor(out=ot[:], in0=ot[:], in1=st[:], op=ALU.add), [otb, stb], [otb])
            C.dma("sp", outT[k * 128:(k + 1) * 128, tc * 512:(tc + 1) * 512], ot[:], [otb], [], chan=f"pout{n % 2}")
            n += 1


def build_mlp():
    C = Ctx(); S = C.S
    xT = C.din("xT", [D, TL]); gpre_d = C.din("gpre", [128, KC]); gpost_d = C.din("gpost", [128, KC])
    w1 = C.din("w1", [D, DFF]); w2 = C.din("w2", [DFF, D])
    outT = C.dout("outT", [D, TL])
    ones, b_ones = emit_consts(C)
    ACC = C.sb([128, KC, TL], F32, "ACC"); b_acc = [Buf(f"acc{k}") for k in range(KC)]
    HT = C.sb([128, KC, TL], BF16, "HT"); b_ht = [Buf(f"ht{k}") for k in range(KC)]
    rbc = C.sb([128, TL], F32, "rbc"); b_rbc = Buf("rbc")
    gpre = C.sb([128, KC], F32, "gpre"); gpost = C.sb([128, KC], F32, "gpost"); b_g = Buf("g")
    sqrot = C.rot(2, [128, 512], BF16, "sq")
    psA = C.psrot(4, "psA"); psB = C.psrot(4, "psB")
    C.dma("sp", gpre[:], gpre_d, [], [b_g], chan="g")
    C.dma("sp", gpost[:], gpost_d, [], [b_g], chan="g")
    for k in range(KC):
        C.dma("sp", ACC[:, k, :], xT[k * 128:(k + 1) * 128, :], [], [b_acc[k]], chan=f"xin{k}")
    chunks = [((lambda c0, w, k=k: ACC[:, k, c0:c0 + w]), b_acc[k]) for k in range(KC)]
    emit_stats(C, ones, b_ones, chunks, TL, rbc, b_rbc, sqrot, psA)
    for k in range(KC):
        S.op("dve", lambda e, k=k: e.scalar_tensor_tensor(out=HT[:, k, :], in0=ACC[:, k, :], scalar=gpre[:, k:k + 1], in1=rbc[:],
                                                          op0=ALU.mult, op1=ALU.mult), [b_acc[k], b_g, b_rbc], [b_ht[k]])
    FB = 512; NFB = DFF // FB; FCB = FB // 128
    W1P = C.rot(4, [128, KC, 256], BF16, "w1p"); W2P = C.rot(4, [128, 2, D], BF16, "w2p")
    HID = C.rot(2, [128, FCB, TL], BF16, "hid"); RT = C.rot(3, [128, 512], F32, "rt")
    w1v = w1.rearrange("(k p) f -> p k f", p=128); w2v = w2.rearrange("(c p) d -> p c d", p=128)
    nw1 = 0; nw2 = 0
    for fb in range(NFB):
        w1p = []; w2p = []
        for h in range(2):
            t, b = W1P.next(); f0 = fb * FB + h * 256
            C.dma("pool", t[:], w1v[:, :, f0:f0 + 256], [], [b], chan=f"w1p{nw1 % 4}"); nw1 += 1
            w1p.append((t, b))
        for h in range(2):
            t, b = W2P.next(); c0 = fb * FCB + h * 2
            C.dma("pool", t[:], w2v[:, c0:c0 + 2, :], [], [b], chan=f"w2p{nw2 % 4}"); nw2 += 1
            w2p.append((t, b))
        hid, hb = HID.next()
        for fc in range(FCB):
            wt, wb = w1p[fc // 2]; fo = (fc % 2) * 128
            for tc in range(2):
                pt, pb = psA.next()

                def mm(e, pt=pt, wt=wt, fo=fo, tc=tc):
                    for k in range(KC):
                        ins = e.matmul(pt[:], wt[:, k, fo:fo + 128], HT[:, k, tc * 512:(tc + 1) * 512], start=(k == 0), stop=(k == KC - 1))
                    return ins
                S.op("pe", mm, [wb] + b_ht, [pb])
                rt, rb = RT.next()
                S.op("act", lambda e, rt=rt, pt=pt: e.activation(out=rt[:], in_=pt[:], func=AF.Relu), [pb], [rb])
                S.op("dve", lambda e, rt=rt, hid=hid, fc=fc, tc=tc: e.tensor_tensor(out=hid[:, fc, tc * 512:(tc + 1) * 512], in0=rt[:], in1=rt[:], op=ALU.mult),
                     [rb], [hb])
        for dc in range(KC):
            for tc in range(2):
                pt, pb = psB.next()

                def mm2(e, pt=pt, dc=dc, tc=tc, hid=hid, w2p=w2p):
                    for fc in range(FCB):
                        wt = w2p[fc // 2][0]
                        ins = e.matmul(pt[:], wt[:, fc % 2, dc * 128:(dc + 1) * 128], hid[:, fc, tc * 512:(tc + 1) * 512], start=(fc == 0), stop=(fc == FCB - 1))
                    return ins
                S.op("pe", mm2, [hb, w2p[0][1], w2p[1][1]], [pb])
                if fb == 0:
                    S.op("act", lambda e, pt=pt, dc=dc, tc=tc: e.activation(out=ACC[:, dc, tc * 512:(tc + 1) * 512], in_=pt[:], func=AF.Copy),
                         [pb], [b_acc[dc]])
                else:
                    S.op("dve", lambda e, pt=pt, dc=dc, tc=tc: e.tensor_tensor(out=ACC[:, dc, tc * 512:(tc + 1) * 512], in0=ACC[:, dc, tc * 512:(tc + 1) * 512],
                                                                             in1=pt[:], op=ALU.add), [pb, b_acc[dc]], [b_acc[dc]])
    emit_post(C, ones, b_ones, ACC, b_acc, TL, 0, gpost, b_g, lambda k: xT[k * 128:(k + 1) * 128, :], outT, sqrot, psA, rbc, b_rbc)
    return C.done()


def build_mix1():
    C = Ctx(); S = C.S
    HL = 8; NCOL = TL + 2 * HL
    xT = C.din("xT", [D, NCOL]); gpre_d = C.din("gpre", [128, KC]); gpost_d = C.din("gpost", [128, KC])
    pw = C.din("pw", [4, 512, 512]); psc_d = C.din("pscale", [128, KC]); invc_d = C.din("invc", [128, 4, TL])
    outT = C.dout("outT", [D, TL])
    ones, b_ones = emit_consts(C)
    ACC = C.sb([128, KC, NCOL], F32, "ACC"); b_acc = [Buf(f"acc{k}") for k in range(KC)]
    D16 = C.sb([128, KC, TL], BF16, "D16"); b_d = [Buf(f"d{k}") for k in range(KC)]
    rbc = C.sb([128, NCOL], F32, "rbc"); b_rbc = Buf("rbc")
    gpre = C.sb([128, KC], F32, "gpre"); gpost = C.sb([128, KC], F32, "gpost"); psc = C.sb([128, KC], F32, "psc"); b_g = Buf("g")
    INVC = C.sb([128, 4, TL], F32, "invc"); b_invc = Buf("invc")
    PW = C.sb([128, 4, 4, 512], BF16, "PW"); b_pw = [Buf(f"pw{g}") for g in range(4)]
    sqrot = C.rot(2, [128, 512], BF16, "sq"); psA = C.psrot(8, "psA")
    C.dma("sp", gpre[:], gpre_d, [], [b_g], chan="g")
    C.dma("sp", gpost[:], gpost_d, [], [b_g], chan="g")
    C.dma("sp", psc[:], psc_d, [], [b_g], chan="g")
    C.dma("sp", INVC[:], invc_d, [], [b_invc], chan="invc")
    for g in range(4):
        C.dma("pool", PW[:, g], pw[g].rearrange("(c p) o -> p c o", p=128), [], [b_pw[g]], chan=f"pw{g}")
    for k in range(KC):
        C.dma("sp", ACC[:, k, :], xT[k * 128:(k + 1) * 128, :], [], [b_acc[k]], chan=f"xin{k}")
    chunks = [((lambda c0, w, k=k: ACC[:, k, c0:c0 + w]), b_acc[k]) for k in range(KC)]
    emit_stats(C, ones, b_ones, chunks, NCOL, rbc, b_rbc, sqrot, psA)
    HC = C.rot(2, [128, NCOL], F32, "hc"); SM = C.rot(2, [128, TL], F32, "sm")
    for k in range(KC):
        g = k // 4; r = (1, 2, 4, 8)[g]
        hc, hb = HC.next(); sm, smb = SM.next()
        S.op("dve", lambda e, k=k, hc=hc: e.scalar_tensor_tensor(out=hc[:], in0=ACC[:, k, :], scalar=gpre[:, k:k + 1], in1=rbc[:],
                                                               op0=ALU.mult, op1=ALU.mult), [b_acc[k], b_g, b_rbc], [hb])
        eng = "pool" if k % 2 == 0 else "dve"
        S.op(eng, lambda e, hc=hc, sm=sm, r=r: e.tensor_tensor(out=sm[:], in0=hc[:, HL - r:HL - r + TL], in1=hc[:, HL - r + 1:HL - r + 1 + TL], op=ALU.add),
             [hb], [smb])
        for j in range(-r + 2, r + 1):
            S.op(eng, lambda e, hc=hc, sm=sm, j=j: e.tensor_tensor(out=sm[:], in0=sm[:], in1=hc[:, HL + j:HL + j + TL], op=ALU.add), [hb, smb], [smb])
        S.op(eng, lambda e, sm=sm, g=g: e.tensor_tensor(out=sm[:], in0=sm[:], in1=INVC[:, g, :], op=ALU.mult), [smb, b_invc], [smb])
        S.op(eng, lambda e, sm=sm, hc=hc, k=k: e.tensor_tensor(out=D16[:, k, :], in0=sm[:], in1=hc[:, HL:HL + TL], op=ALU.subtract), [smb, hb], [b_d[k]])
    for g in range(4):
        for dco in range(4):
            dc = 4 * g + dco
            for tc in range(2):
                pt, pb = psA.next()

                def mm(e, pt=pt, g=g, dco=dco, tc=tc):
                    for cc in range(4):
                        ins = e.matmul(pt[:], PW[:, g, cc, dco * 128:(dco + 1) * 128], D16[:, 4 * g + cc, tc * 512:(tc + 1) * 512], start=(cc == 0), stop=(cc == 3))
                    return ins
                S.op("pe", mm, [b_pw[g]] + b_d[4 * g:4 * g + 4], [pb])
                S.op("dve", lambda e, pt=pt, dc=dc, tc=tc: e.tensor_scalar(out=ACC[:, dc, HL + tc * 512:HL + (tc + 1) * 512], in0=pt[:], scalar1=psc[:, dc:dc + 1],
                                                                          scalar2=None, op0=ALU.mult), [pb, b_g], [b_acc[dc]])
    emit_post(C, ones, b_ones, ACC, b_acc, NCOL, HL, gpost, b_g, lambda k: xT[k * 128:(k + 1) * 128, HL:HL + TL], outT, sqrot, psA, rbc, b_rbc)
    return C.done()


def build_mix0():
    C = Ctx(); S = C.S
    NCOL = TL + 4
    xT = C.din("xT", [D, NCOL]); gpre_d = C.din("gpre", [128, KC]); gpost_d = C.din("gpost", [128, KC])
    yaT = C.din("yaT", [1024, TL], BF16); w_in = C.din("w_in", [D, 6144]); scw_d = C.din("scw", [128, 3, 8]); w_out = C.din("w_out", [D, D])
    outT = C.dout("outT", [D, TL])
    ones, b_ones = emit_consts(C)
    ACC = C.sb([128, KC, NCOL], F32, "ACC"); b_acc = [Buf(f"acc{k}") for k in range(KC)]
    HT = C.sb([128, KC, NCOL], BF16, "HT"); b_ht = [Buf(f"ht{k}") for k in range(KC)]
    Y = C.sb([128, KC, TL], BF16, "Y"); b_y = [Buf(f"y{k}") for k in range(KC)]
    rbc = C.sb([128, NCOL], F32, "rbc"); b_rbc = Buf("rbc")
    gpre = C.sb([128, KC], F32, "gpre"); gpost = C.sb([128, KC], F32, "gpost"); scw = C.sb([128, 3, 8], F32, "scw"); b_g = Buf("g")
    sqrot = C.rot(2, [128, 512], BF16, "sq"); psA = C.psrot(8, "psA")
    C.dma("sp", gpre[:], gpre_d, [], [b_g], chan="g")
    C.dma("sp", gpost[:], gpost_d, [], [b_g], chan="g")
    C.dma("sp", scw[:], scw_d, [], [b_g], chan="g")
    for k in range(KC):
        C.dma("sp", ACC[:, k, :], xT[k * 128:(k + 1) * 128, :], [], [b_acc[k]], chan=f"xin{k}")
    for j in range(8):
        C.dma("sp", Y[:, j, :], yaT[j * 128:(j + 1) * 128, :], [], [b_y[j]], chan=f"ya{j}")
    chunks = [((lambda c0, w, k=k: ACC[:, k, c0:c0 + w]), b_acc[k]) for k in range(KC)]
    emit_stats(C, ones, b_ones, chunks, NCOL, rbc, b_rbc, sqrot, psA)
    for k in range(KC):
        S.op("dve", lambda e, k=k: e.scalar_tensor_tensor(out=HT[:, k, :], in0=ACC[:, k, :], scalar=gpre[:, k:k + 1], in1=rbc[:],
                                                          op0=ALU.mult, op1=ALU.mult), [b_acc[k], b_g, b_rbc], [b_ht[k]])
    WIN = [(C.sb([128, KC, 384], BF16, f"win{i}"), [Buf(f"win{i}_{s}") for s in range(3)]) for i in range(2)]
    TA = C.rot(1, [128, NCOL], F32, "ta"); PP = C.rot(2, [128, NCOL], F32, "pp"); CV = C.rot(2, [128, TL], F32, "cv")
    CH = cchunks(NCOL)
    for i in range(8):
        wt, wbs = WIN[i % 2]
        for s in range(3):
            col = 3072 + s * 1024 + i * 128
            C.dma("pool", wt[:, :, s * 128:(s + 1) * 128], w_in[:, col:col + 128].rearrange("(k p) c -> p k c", p=128), [], [wbs[s]], chan=f"win{i % 2}_{s}")
        ta, tab = TA.next(); pp, ppb = PP.next(); cv, cvb = CV.next()
        for (c0, w) in CH:
            pg, pgb = psA.next(); px, pxb = psA.next()
            for (pt, pb, s) in ((pg, pgb, 1), (px, pxb, 2)):
                def mm(e, pt=pt, s=s, c0=c0, w=w, wt=wt):
                    for k in range(KC):
                        ins = e.matmul(pt[:, :w], wt[:, k, s * 128:(s + 1) * 128], HT[:, k, c0:c0 + w], start=(k == 0), stop=(k == KC - 1))
                    return ins
                S.op("pe", mm, [wbs[s]] + b_ht, [pb])
            S.op("act", lambda e, ta=ta, pg=pg, c0=c0, w=w: e.activation(out=ta[:, c0:c0 + w], in_=pg[:, :w], func=AF.Copy), [pgb], [tab])
            S.op("dve", lambda e, ta=ta, px=px, pp=pp, c0=c0, w=w: e.tensor_tensor(out=pp[:, c0:c0 + w], in0=ta[:, c0:c0 + w], in1=px[:, :w], op=ALU.mult),
                 [tab, pxb], [ppb])
        S.op("dve", lambda e, cv=cv, pp=pp, i=i: e.tensor_scalar(out=cv[:], in0=pp[:, 2:2 + TL], scalar1=scw[:, 1, i:i + 1], scalar2=None, op0=ALU.mult), [ppb, b_g], [cvb])
        S.op("dve", lambda e, cv=cv, pp=pp, i=i: e.scalar_tensor_tensor(out=cv[:], in0=pp[:, 1:1 + TL], scalar=scw[:, 0, i:i + 1], in1=cv[:], op0=ALU.mult, op1=ALU.add),
             [ppb, b_g, cvb], [cvb])
        S.op("dve", lambda e, cv=cv, pp=pp, i=i: e.scalar_tensor_tensor(out=cv[:], in0=pp[:, 3:3 + TL], scalar=scw[:, 2, i:i + 1], in1=cv[:], op0=ALU.mult, op1=ALU.add),
             [ppb, b_g, cvb], [cvb])
        for tc in range(2):
            pt, pb = psA.next()

            def mmb(e, pt=pt, tc=tc, wt=wt):
                for k in range(KC):
                    ins = e.matmul(pt[:], wt[:, k, 0:128], HT[:, k, 2 + tc * 512:2 + (tc + 1) * 512], start=(k == 0), stop=(k == KC - 1))
                return ins
            S.op("pe", mmb, [wbs[0]] + b_ht, [pb])
            S.op("dve", lambda e, pt=pt, cv=cv, i=i, tc=tc: e.tensor_tensor(out=Y[:, 8 + i, tc * 512:(tc + 1) * 512], in0=pt[:], in1=cv[:, tc * 512:(tc + 1) * 512], op=ALU.mult),
                 [pb, cvb], [b_y[8 + i]])
    WOUT = C.rot(2, [128, KC, 256], BF16, "wout")
    wov = w_out.rearrange("(c p) d -> p c d", p=128)
    for pc in range(8):
        wt, wb = WOUT.next()
        C.dma("pool", wt[:], wov[:, :, pc * 256:(pc + 1) * 256], [], [wb], chan=f"wout{pc % 2}")
        for j in range(2):
            dc = pc * 2 + j
            for tc in range(2):
                pt, pb = psA.next()

                def mmo(e, pt=pt, wt=wt, j=j, tc=tc):
                    for cc in range(KC):
                        ins = e.matmul(pt[:], wt[:, cc, j * 128:(j + 1) * 128], Y[:, cc, tc * 512:(tc + 1) * 512], start=(cc == 0), stop=(cc == KC - 1))
                    return ins
                S.op("pe", mmo, [wb] + b_y, [pb])
                S.op("act", lambda e, pt=pt, dc=dc, tc=tc: e.activation(out=ACC[:, dc, 2 + tc * 512:2 + (tc + 1) * 512], in_=pt[:], func=AF.Copy), [pb], [b_acc[dc]])
    emit_post(C, ones, b_ones, ACC, b_acc, NCOL, 2, gpost, b_g, lambda k: xT[k * 128:(k + 1) * 128, 2:2 + TL], outT, sqrot, psA, rbc, b_rbc)
    return C.done()


NF = 2 * L
TB = 256


class Carver:
    def __init__(self, ap, ncols):
        self.ap = ap; self.n = ncols; self.off = 0

    def take(self, n, parts=128):
        v = self.ap[:parts, self.off:self.off + n]; self.off += n
        assert self.off <= self.n, (self.off, self.n)
        return v

    def rot(self, k, n, name, parts=128):
        return Rot([(self.take(n, parts), Buf(f"{name}{i}")) for i in range(k)])


def build_hyena():
    C = Ctx(); S = C.S
    xTf = C.din("xTf", [D, L]); gpre_d = C.din("gpre", [128, KC]); w3x = C.din("w3x", [D, 384])
    hsw_d = C.din("hsw", [128, 3, 3]); hsb_d = C.din("hsb", [128, 3]); hbias_d = C.din("hbias", [128, 1])
    fw1_d = C.din("fw1", [33, 64]); fw2_d = C.din("fw2", [64, 64]); fb_d = C.din("fb", [64, 3])
    w3f_d = C.din("w3f", [64, 128]); w3b_d = C.din("w3b", [64, 128]); b3_d = C.din("b3", [128, 2])
    E_d = C.din("Efull", [33, NF]); Wn_d = C.din("Wn", [128, NF])
    cst16_d = C.din("cst16", [128, 9 * 128], BF16)
    cst32_d = C.din("cst32", [128, 3 * 512])
    yaT = C.dout("yaT", [128, L], BF16)
    Ud = C.dscratch("Ud", [128, L], BF16); Kd = C.dscratch("Kd", [128, NF], BF16)
    XkD = C.dscratch("XkD", [128, 2, NF], BF16); Yd = C.dscratch("Yd", [128, L], F32)
    ones, b_ones = emit_consts(C)
    ZW = L + 2
    A32 = C.sb([128, 3 * ZW + L], F32, "A32")
    A16 = C.sb([128, L], BF16, "A16")
    Zs = [A32[:, s * ZW:(s + 1) * ZW] for s in range(3)]; b_z = [Buf(f"z{s}") for s in range(3)]
    XSR = A32[:, 3 * ZW:3 * ZW + L]
    gpre = C.sb([128, KC], F32, "gpre"); hsw = C.sb([128, 3, 3], F32, "hsw"); hsb = C.sb([128, 3], F32, "hsb"); hbias = C.sb([128, 1], F32, "hbias")
    fw1 = C.sb([33, 64], F32, "fw1"); fw2 = C.sb([64, 64], F32, "fw2"); fb = C.sb([64, 3], F32, "fb")
    w3f = C.sb([64, 128], F32, "w3f"); w3b = C.sb([64, 128], F32, "w3b"); b3 = C.sb([128, 2], F32, "b3")
    cst16 = C.sb([128, 9 * 128], BF16, "cst16"); cst32 = C.sb([128, 3 * 512], F32, "cst32"); negpi = C.sb([64, 1], F32, "negpi")
    b_c = Buf("consts")
    for (t, d) in ((gpre, gpre_d), (hsw, hsw_d), (hsb, hsb_d), (hbias, hbias_d), (fw1, fw1_d), (fw2, fw2_d), (fb, fb_d), (w3f, w3f_d),
                   (w3b, w3b_d), (b3, b3_d), (cst16, cst16_d), (cst32, cst32_d)):
        C.dma("sp", t[:], d, [], [b_c], chan="consts")
    S.op("dve", lambda e: e.memset(negpi[:], -math.pi), [], [b_c])
    Fc = cst16[:, 0:128]; Fs = cst16[:, 128:256]; FrS = cst16[:, 256:384]; FiS = cst16[:, 384:512]; nFiS = cst16[:, 512:640]
    Gc = cst16[:, 640:768]; Gs = cst16[:, 768:896]; nGs = cst16[:, 896:1024]; Gr64 = cst16[:, 1024:1088]; nGi64 = cst16[:, 1088:1152]
    TR = cst32[:, 0:512]; TI = cst32[:, 512:1024]; nTI = cst32[:, 1024:1536]
    W3 = C.sb([128, KC, 384], BF16, "W3"); b_w3 = Buf("w3")
    C.dma("pool", W3[:], w3x.rearrange("(k p) c -> p k c", p=128), [], [b_w3], chan="w3")
    for s in range(3):
        S.op("pool", lambda e, s=s: e.memset(Zs[s][:, 0:1], 0.0), [], [b_z[s]])
        S.op("pool", lambda e, s=s: e.memset(Zs[s][:, ZW - 1:ZW], 0.0), [], [b_z[s]])
    ps = C.psrot(8, "ps")
    SQ = C.sb([128, KC, TB], BF16, "SQ"); b_sq = Buf("sq")
    RB = C.rot(2, [128, TB], F32, "rb")
    xv = xTf.rearrange("(k p) t -> p k t", p=128)
    XS = [(XSR[:, i * KC * TB:(i + 1) * KC * TB], Buf(f"xs{i}")) for i in range(2)]
    HTB = [(A16[:, i * KC * TB:(i + 1) * KC * TB], Buf(f"htb{i}")) for i in range(2)]
    for blk in range(L // TB):
        xs, xb = XS[blk % 2]; ht, hb_ = HTB[blk % 2]
        xs3 = xs.rearrange("p (k t) -> p k t", k=KC); ht3 = ht.rearrange("p (k t) -> p k t", k=KC)
        C.dma("sp", xs3, xv[:, :, blk * TB:(blk + 1) * TB], [], [xb], chan=f"xs{blk % 2}")
        S.op("act", lambda e, xs=xs: e.activation(out=SQ[:].rearrange("p k t -> p (k t)"), in_=xs, func=AF.Square), [xb], [b_sq])
        pt, pb = ps.next()

        def mms(e, pt=pt):
            for k in range(KC):
                ins = e.matmul(pt[:, :TB], ones[:], SQ[:, k, :], start=(k == 0), stop=(k == KC - 1))
            return ins
        S.op("pe", mms, [b_sq, b_ones], [pb])
        rb, rbb = RB.next()
        S.op("dve", lambda e, pt=pt, rb=rb: e.tensor_scalar(out=rb[:], in0=pt[:, :TB], scalar1=1.0 / D, scalar2=EPS, op0=ALU.mult, op1=ALU.add), [pb], [rbb])
        S.op("act", lambda e, rb=rb: e.activation(out=rb[:], in_=rb[:], func=AF.Sqrt), [rbb], [rbb])
        S.op("dve", lambda e, rb=rb: e.reciprocal(out=rb[:], in_=rb[:]), [rbb], [rbb])
        for k in range(KC):
            S.op("dve", lambda e, k=k, xs3=xs3, ht3=ht3, rb=rb: e.scalar_tensor_tensor(out=ht3[:, k, :], in0=xs3[:, k, :], scalar=gpre[:, k:k + 1], in1=rb[:],
                                                                                  op0=ALU.mult, op1=ALU.mult), [xb, rbb, b_c], [hb_])
        for s in range(3):
            pt, pb = ps.next()

            def mmz(e, pt=pt, s=s, ht3=ht3):
                for k in range(KC):
                    ins = e.matmul(pt[:, :TB], W3[:, k, s * 128:(s + 1) * 128], ht3[:, k, :], start=(k == 0), stop=(k == KC - 1))
                return ins
            S.op("pe", mmz, [hb_, b_w3], [pb])
            S.op("act", lambda e, pt=pt, s=s, blk=blk: e.activation(out=Zs[s][:, 1 + blk * TB:1 + (blk + 1) * TB], in_=pt[:, :TB], func=AF.Copy), [pb], [b_z[s]])
    S.barrier()
    U32 = XSR; b_u32 = Buf("u32"); U16 = A16; b_u16 = Buf("u16")
    X0 = Zs[2][:, 0:L]
    CW = 1024
    TMP = C.rot(2, [128, CW], F32, "ctmp")

    def conv(s, out, c0, rd, wr):
        S.op("dve", lambda e: e.tensor_scalar(out=out, in0=Zs[s][:, 1 + c0:1 + c0 + CW], scalar1=hsw[:, 1, s:s + 1], scalar2=hsb[:, s:s + 1],
                                              op0=ALU.mult, op1=ALU.add), rd + [b_c], wr)
        S.op("dve", lambda e: e.scalar_tensor_tensor(out=out, in0=Zs[s][:, c0:c0 + CW], scalar=hsw[:, 0, s:s + 1], in1=out, op0=ALU.mult, op1=ALU.add),
             rd + wr + [b_c], wr)
        S.op("dve", lambda e: e.scalar_tensor_tensor(out=out, in0=Zs[s][:, 2 + c0:2 + c0 + CW], scalar=hsw[:, 2, s:s + 1], in1=out, op0=ALU.mult, op1=ALU.add),
             rd + wr + [b_c], wr)
    for j in range(L // CW):
        c0 = j * CW
        ta, tab = TMP.next(); tb_, tbb = TMP.next()
        conv(2, ta[:], c0, [b_z[2]], [tab])
        conv(1, tb_[:], c0, [b_z[1]], [tbb])
        S.op("pool", lambda e, ta=ta, tb_=tb_, c0=c0: e.tensor_tensor(out=U32[:, c0:c0 + CW], in0=ta[:], in1=tb_[:], op=ALU.mult), [tab, tbb], [b_u32])
        S.op("act", lambda e, c0=c0: e.activation(out=U16[:, c0:c0 + CW], in_=U32[:, c0:c0 + CW], func=AF.Copy), [b_u32], [b_u16])
    b_x0 = b_z[2]
    for j in range(L // CW):
        c0 = j * CW
        ta, tab = TMP.next()
        conv(0, ta[:], c0, [b_z[0]], [tab])
        S.op("pool", lambda e, ta=ta, c0=c0: e.tensor_copy(out=X0[:, c0:c0 + CW], in_=ta[:]), [tab], [b_x0])
    b_ud = Buf("ud")
    C.dma("sp", Ud, U16[:], [b_u16], [b_ud], chan="ud")
    S.barrier()
    cv32 = Carver(A32, 2 * ZW); cv16 = Carver(A16, L)
    ECH = cv32.rot(2, 512, "ech", 33); H1 = cv32.rot(2, 512, "h1", 64); H2 = cv32.rot(2, 512, "h2", 64)
    WNC = cv32.rot(2, 512, "wnc"); KCH = cv16.rot(2, 512, "kch"); RR = cv32.rot(2, 512, "rr", 64)
    MAGIC = 12582912.0
    for mch in range(NF // 512):
        fwd = mch < (L // 512)
        ec, ecb = ECH.next(); wn, wnb = WNC.next()
        C.dma("sp", ec[:], E_d[:, mch * 512:(mch + 1) * 512], [], [ecb], chan=f"ech{mch % 2}")
        C.dma("sp", wn[:], Wn_d[:, mch * 512:(mch + 1) * 512], [], [wnb], chan=f"wnc{mch % 2}")
        src = (ec, ecb, 33, fw1); hcur = None
        for li, (HR, col) in enumerate(((H1, 0), (H2, 1))):
            pt, pb = ps.next()
            a, ab, kk, wmat = src
            S.op("pe", lambda e, pt=pt, a=a, kk=kk, wmat=wmat: e.matmul(pt[:64, :], wmat[:kk, :], a[:kk, :], start=True, stop=True), [ab, b_c], [pb])
            h, hb2 = HR.next()
            S.op("dve", lambda e, pt=pt, h=h, col=col: e.tensor_scalar(out=h[:], in0=pt[:64, :], scalar1=fb[:, col:col + 1], scalar2=fb[:, 2:3], op0=ALU.add, op1=ALU.mult),
                 [pb, b_c], [hb2])
            rr, rrb = RR.next()
            S.op("dve", lambda e, h=h, rr=rr: e.tensor_scalar(out=rr[:], in0=h[:], scalar1=1.0 / (2.0 * math.pi), scalar2=MAGIC, op0=ALU.mult, op1=ALU.add), [hb2], [rrb])
            S.op("dve", lambda e, rr=rr: e.tensor_scalar(out=rr[:], in0=rr[:], scalar1=MAGIC, scalar2=None, op0=ALU.subtract), [rrb], [rrb])
            S.op("dve", lambda e, h=h, rr=rr: e.scalar_tensor_tensor(out=h[:], in0=h[:], scalar=1.0 / (2.0 * math.pi), in1=rr[:], op0=ALU.mult, op1=ALU.subtract),
                 [hb2, rrb], [hb2])
            S.op("act", lambda e, h=h: e.activation(out=h[:], in_=h[:], func=AF.Sin, scale=6.28318), [hb2], [hb2])
            src = (h, hb2, 64, fw2)
        h, hb2 = src[0], src[1]
        pt, pb = ps.next()
        w3m = w3f if fwd else w3b
        S.op("pe", lambda e, pt=pt, h=h, w3m=w3m: e.matmul(pt[:, :], w3m[:, :], h[:, :], start=True, stop=True), [hb2, b_c], [pb])
        kc_, kcb = KCH.next()
        bcol = 0 if fwd else 1
        S.op("dve", lambda e, pt=pt, kc_=kc_, wn=wn, bcol=bcol: e.scalar_tensor_tensor(out=kc_[:], in0=pt[:], scalar=b3[:, bcol:bcol + 1], in1=wn[:], op0=ALU.add, op1=ALU.mult),
             [pb, wnb, b_c], [kcb])
        C.dma("sp", Kd[:, mch * 512:(mch + 1) * 512], kc_[:], [kcb], [], chan=f"kd{mch % 2}")
    S.barrier()
    DT = cv16.rot(2, 512, "dt"); T4 = [cv32.rot(2, 512, f"t{i}") for i in range(4)]
    BR = cv16.rot(2, 512, "br"); BI = cv16.rot(2, 512, "bi")
    w3flat = W3[:].rearrange("p k c -> p (k c)")
    XKS = Rot([(w3flat[:, i * 1024:(i + 1) * 1024].rearrange("p (r n) -> p r n", r=2), Buf(f"xks{i}")) for i in range(2)])

    def cmul(pr, prb, pi, pib, ar, ai, rdb, outr, outrb, outi, outib, conj=False):
        (t1, t1b), (t2, t2b), (t3, t3b), (t4, t4b) = [T4[i].next() for i in range(4)]
        S.op("dve", lambda e: e.tensor_tensor(out=t1[:], in0=pr[:], in1=ar, op=ALU.mult), [prb] + rdb, [t1b])
        S.op("dve", lambda e: e.tensor_tensor(out=t2[:], in0=pi[:], in1=ai, op=ALU.mult), [pib] + rdb, [t2b])
        S.op("dve", lambda e: e.tensor_tensor(out=t3[:], in0=pr[:], in1=ai, op=ALU.mult), [prb] + rdb, [t3b])
        S.op("dve", lambda e: e.tensor_tensor(out=t4[:], in0=pi[:], in1=ar, op=ALU.mult), [pib] + rdb, [t4b])
        if not conj:
            S.op("pool", lambda e: e.tensor_tensor(out=outr, in0=t1[:], in1=t2[:], op=ALU.subtract), [t1b, t2b], [outrb])
            S.op("pool", lambda e: e.tensor_tensor(out=outi, in0=t3[:], in1=t4[:], op=ALU.add), [t3b, t4b], [outib])
        else:
            S.op("pool", lambda e: e.tensor_tensor(out=outr, in0=t1[:], in1=t2[:], op=ALU.add), [t1b, t2b], [outrb])
            S.op("pool", lambda e: e.tensor_tensor(out=outi, in0=t4[:], in1=t3[:], op=ALU.subtract), [t3b, t4b], [outib])

    def fwd_fft(src_d, cg, nbk):
        dt, dtb = DT.next()
        dt3 = dt[:nbk, :].rearrange("p (c a) -> p c a", c=4)
        C.dma("sp", dt3, src_d[4 * cg:4 * cg + 4, 0:nbk * 128].rearrange("c (nb na) -> nb c na", na=128), [], [dtb], chan=f"dt{(DT.i - 1) % 2}")
        pr, prb = ps.next(); pi, pib = ps.next()

        def s1(e, pt, Fm):
            for j in range(4):
                ins = e.matmul(pt[:, j * 128:(j + 1) * 128], dt[:nbk, j * 128:(j + 1) * 128], Fm[:nbk, :], start=True, stop=True)
            return ins
        S.op("pe", lambda e: s1(e, pr, Fc), [dtb, b_c], [prb])
        S.op("pe", lambda e: s1(e, pi, Fs), [dtb, b_c], [pib])
        br, brb = BR.next(); bi, bib = BI.next()
        cmul(pr, prb, pi, pib, TR, TI, [b_c], br[:], brb, bi[:], bib)
        pxr, pxrb = ps.next(); pxi, pxib = ps.next()

        def s2r(e):
            e.matmul(pxr[:], FrS, br[:], start=True, stop=False)
            return e.matmul(pxr[:], nFiS, bi[:], start=False, stop=True)

        def s2i(e):
            e.matmul(pxi[:], FiS, br[:], start=True, stop=False)
            return e.matmul(pxi[:], FrS, bi[:], start=False, stop=True)
        S.op("pe", s2r, [brb, bib, b_c], [pxrb])
        S.op("pe", s2i, [brb, bib, b_c], [pxib])
        return (pxr, pxrb), (pxi, pxib)

    for cg in range(32):
        (pxr, pxrb), (pxi, pxib) = fwd_fft(Kd, cg, 128)
        xk, xkb = XKS.next()
        S.op("act", lambda e, xk=xk, pxr=pxr: e.activation(out=xk[:, 0, :], in_=pxr[:], func=AF.Copy), [pxrb], [xkb])
        S.op("act", lambda e, xk=xk, pxi=pxi: e.activation(out=xk[:, 1, :], in_=pxi[:], func=AF.Copy), [pxib], [xkb])
        C.dma("sp", XkD[:, :, cg * 512:(cg + 1) * 512], xk[:], [xkb], [], chan=f"xkd{cg % 2}")
    S.barrier()
    ZR = cv16.rot(2, 512, "zr"); ZI = cv16.rot(2, 512, "zi")
    DR = cv16.rot(2, 512, "dr"); DI = cv16.rot(2, 512, "di")
    YS = cv32.rot(2, 512, "ys", 64)
    for cg in range(32):
        xk, xkb = XKS.next()
        C.dma("sp", xk[:], XkD[:, :, cg * 512:(cg + 1) * 512], [], [xkb], chan=f"xkl{cg % 2}")
        (pxr, pxrb), (pxi, pxib) = fwd_fft(Ud, cg, 64)
        zr, zrb = ZR.next(); zi, zib = ZI.next()
        cmul(pxr, pxrb, pxi, pxib, xk[:, 0, :], xk[:, 1, :], [xkb], zr[:], zrb, zi[:], zib)
        pcr, pcrb = ps.next(); pci, pcib = ps.next()

        def s3(e, pt, Ma, Mb, zr=zr, zi=zi):
            for j in range(4):
                e.matmul(pt[:, j * 128:(j + 1) * 128], zr[:, j * 128:(j + 1) * 128], Ma, start=True, stop=False)
                ins = e.matmul(pt[:, j * 128:(j + 1) * 128], zi[:, j * 128:(j + 1) * 128], Mb, start=False, stop=True)
            return ins
        S.op("pe", lambda e, pcr=pcr, s3=s3: s3(e, pcr, Gc, nGs), [zrb, zib, b_c], [pcrb])
        S.op("pe", lambda e, pci=pci, s3=s3: s3(e, pci, Gs, Gc), [zrb, zib, b_c], [pcib])
        dr, drb = DR.next(); di, dib = DI.next()
        cmul(pcr, pcrb, pci, pcib, TR, TI, [b_c], dr[:], drb, di[:], dib, conj=True)
        py, pyb = ps.next()

        def s4(e, py=py, dr=dr, di=di):
            e.matmul(py[:64, :], Gr64, dr[:], start=True, stop=False)
            return e.matmul(py[:64, :], nGi64, di[:], start=False, stop=True)
        S.op("pe", s4, [drb, dib, b_c], [pyb])
        ys, ysb = YS.next()
        S.op("act", lambda e, ys=ys, py=py: e.activation(out=ys[:], in_=py[:64, :], func=AF.Copy), [pyb], [ysb])
        C.dma("sp", Yd[4 * cg:4 * cg + 4, :].rearrange("c (nb na) -> nb c na", na=128), ys[:].rearrange("p (c a) -> p c a", c=4), [ysb], [], chan=f"yd{cg % 2}")
    S.barrier()
    YC = cv32.rot(2, CW, "yc")
    sqflat = SQ[:].rearrange("p k t -> p (k t)")
    YO = Rot([(sqflat[:, i * CW:(i + 1) * CW], Buf(f"yo{i}")) for i in range(2)])
    for j in range(L // CW):
        c0 = j * CW
        yc, ycb = YC.next(); yo, yob = YO.next()
        C.dma("sp", yc[:], Yd[:, c0:c0 + CW], [], [ycb], chan=f"yc{j % 2}")
        S.op("dve", lambda e, yc=yc, c0=c0: e.scalar_tensor_tensor(out=yc[:], in0=U32[:, c0:c0 + CW], scalar=hbias[:, 0:1], in1=yc[:], op0=ALU.mult, op1=ALU.add),
             [ycb, b_u32, b_c], [ycb])
        S.op("pool", lambda e, yc=yc, yo=yo, c0=c0: e.tensor_tensor(out=yo[:], in0=yc[:], in1=X0[:, c0:c0 + CW], op=ALU.mult), [ycb, b_x0], [yob])
        C.dma("sp", yaT[:, c0:c0 + CW], yo[:], [yob], [], chan=f"yao{j % 2}")
    return C.done()


def hyena_consts():
    n = np.arange(128, dtype=np.float64)
    ang = 2 * np.pi * np.outer(n, n) / 128.0
    cos, sin = np.cos(ang), np.sin(ang)
    angN = 2 * np.pi * np.outer(n, n) / NF
    Tr, Ti = np.cos(angN), -np.sin(angN)
    c16 = np.concatenate([cos, -sin, cos, -sin, sin, cos, sin, -sin, cos[:, :64] / NF, -sin[:, :64] / NF], 1)
    c32 = np.concatenate([np.tile(Tr, (1, 4)), np.tile(Ti, (1, 4)), np.tile(-Ti, (1, 4))], 1)
    t = np.linspace(0.0, 1.0, L)
    wpos = 2 * np.pi * np.arange(L) / L
    bands = np.linspace(1e-4, 15.0, 16)
    a = np.outer(wpos, bands)
    emb = np.concatenate([t[:, None], np.cos(a), -np.sin(a)], 1)
    idx = np.concatenate([np.arange(L), [0], L - np.arange(1, L)])
    E = emb[idx].T
    mx = math.log(1e-2) / 0.3; mn = math.log(1e-2) / 1.5
    deltas = np.abs(np.linspace(mn, mx, 1024))
    win = np.exp(-t[idx][None, :] * deltas[:, None])
    win[:, L] = 0.0
    return c16.astype(ml_dtypes.bfloat16), c32.astype(np.float32), np.ascontiguousarray(E, np.float32), win.astype(np.float32)


def hyena_maps(xT_full, g0, w_in, hsw, hsb, fw1, fb1, fw2, fb2, fw3, fb3, freq, hbias):
    c16, c32, E, win = hyena_consts()
    maps = []
    for c in range(NC):
        cs = [s * 1024 + c * 128 for s in range(3)]
        w3x = np.ascontiguousarray(np.concatenate([w_in[:, a:a + 128] for a in cs], 1))
        m = {"xTf": xT_full, "gpre": g_lay(g0), "w3x": w3x,
             "hsw": np.ascontiguousarray(np.stack([hsw[:, a:a + 128] for a in cs], -1).transpose(1, 0, 2)),
             "hsb": np.ascontiguousarray(np.stack([hsb[a:a + 128] for a in cs], -1)),
             "hbias": np.ascontiguousarray(hbias[c * 128:(c + 1) * 128, None]),
             "fw1": fw1, "fw2": fw2, "fb": np.ascontiguousarray(np.stack([fb1, fb2, freq], -1)),
             "w3f": np.ascontiguousarray(fw3[:, c * 128:(c + 1) * 128]), "w3b": np.ascontiguousarray(fw3[:, 1024 + c * 128:1024 + (c + 1) * 128]),
             "b3": np.ascontiguousarray(np.stack([fb3[c * 128:(c + 1) * 128], fb3[1024 + c * 128:1024 + (c + 1) * 128]], -1)),
             "Efull": E, "Wn": np.ascontiguousarray(win[c * 128:(c + 1) * 128]), "cst16": c16, "cst32": c32}
        maps.append({k: np.ascontiguousarray(v) for k, v in m.items()})
    return maps


def g_lay(g):
    return np.ascontiguousarray(np.asarray(g, np.float32).reshape(KC, 128).T)


def run_prog(nc, maps, outname="outT"):
    res = run_bass_kernel_spmd(nc, maps, core_ids=list(range(NC)))
    return [np.asarray(res.results[c][outname]) for c in range(NC)]


def halo_T(x, h):
    xp = np.zeros((L + 2 * h, D), np.float32); xp[h:h + L] = x
    return [np.ascontiguousarray(xp[c * TL:c * TL + TL + 2 * h].T) for c in range(NC)]


def shards_to_full(outs):
    return np.concatenate([o.T for o in outs], 0)


def kernel(x, norm_g, mix_w_in, hy_short_w, hy_short_b, hy_filt_w1, hy_filt_b1, hy_filt_w2, hy_filt_b2, hy_filt_w3, hy_filt_b3,
           hy_freq, hy_bias, sc_conv_w, mix_w_out, pool_w, pool_scale, mlp_w1, mlp_w2):
    f = lambda a: np.ascontiguousarray(np.asarray(a, dtype=np.float32))
    x = f(x)[0]; norm_g = f(norm_g)
    w_in = f(mix_w_in)[0]
    maps = hyena_maps(np.ascontiguousarray(x.T), norm_g[0, 0], w_in, f(hy_short_w)[0], f(hy_short_b)[0], f(hy_filt_w1)[0], f(hy_filt_b1)[0],
                      f(hy_filt_w2)[0], f(hy_filt_b2)[0], f(hy_filt_w3)[0], f(hy_filt_b3)[0], f(hy_freq)[0], f(hy_bias)[0])
    ya = run_prog(build_hyena(), maps, "yaT")
    yaT = np.concatenate(ya, 0)
    xs = halo_T(x, 2)
    scl = np.ascontiguousarray(f(sc_conv_w)[0].reshape(3, 8, 128).transpose(2, 0, 1))
    w_out = f(mix_w_out)[0]
    maps = [{"xT": xs[c], "gpre": g_lay(norm_g[0, 0]), "gpost": g_lay(norm_g[0, 1]), "yaT": np.ascontiguousarray(yaT[:, c * TL:(c + 1) * TL]),
             "w_in": w_in, "scw": scl, "w_out": w_out} for c in range(NC)]
    x1 = run_prog(build_mix0(), maps)
    w1 = f(mlp_w1); w2 = f(mlp_w2)
    maps = [{"xT": x1[c], "gpre": g_lay(norm_g[0, 2]), "gpost": g_lay(norm_g[0, 3]), "w1": w1[0], "w2": w2[0]} for c in range(NC)]
    x2 = run_prog(build_mlp(), maps)
    xs = halo_T(shards_to_full(x2), 8)
    t = np.arange(L)
    invc = np.stack([1.0 / (np.minimum(t + w // 2 + 1, L) - np.maximum(t - w // 2, 0)) for w in (2, 4, 8, 16)], 0).astype(np.float32)
    maps = [{"xT": xs[c], "gpre": g_lay(norm_g[1, 0]), "gpost": g_lay(norm_g[1, 1]), "pw": f(pool_w)[0], "pscale": g_lay(f(pool_scale)[0]),
             "invc": np.ascontiguousarray(np.broadcast_to(invc[None, :, c * TL:(c + 1) * TL], (128, 4, TL)))} for c in range(NC)]
    x3 = run_prog(build_mix1(), maps)
    maps = [{"xT": x3[c], "gpre": g_lay(norm_g[1, 2]), "gpost": g_lay(norm_g[1, 3]), "w1": w1[1], "w2": w2[1]} for c in range(NC)]
    x4 = run_prog(build_mlp(), maps)
    return np.ascontiguousarray(shards_to_full(x4)[None].astype(np.float32))
```
